# Optimizing a Trainium2 kernel written in Bass

```python
import math
import jax, jax.numpy as jnp
from jax import lax
import numpy as np

D_MODEL = 2048
BATCH = 4
SEQ = 2048
DEPTH = 4

CHUNK = 64
EPS = 1e-6
N_EVEN = (DEPTH + 1) // 2
N_ODD = DEPTH // 2

GM_BLOCK = 128
GM_GROUPS = 8
GM_WIDTH = D_MODEL
GM_GROUP_DIM = GM_WIDTH // GM_GROUPS
SSM_D_INNER = D_MODEL
SSM_HEAD_DIM = 64
SSM_HEADS = SSM_D_INNER // SSM_HEAD_DIM
SSM_GROUPS = 4
SSM_HEADS_PER_GROUP = SSM_HEADS // SSM_GROUPS
SSM_STATE = 128
SSM_CONV = 4
SSM_CHUNK = CHUNK
SSM_CONV_DIM = SSM_D_INNER + 2 * SSM_GROUPS * SSM_STATE
EVEN_IN = 2 * GM_WIDTH + SSM_D_INNER + SSM_CONV_DIM + SSM_HEADS
EVEN_MIX = GM_WIDTH + SSM_D_INNER
MLA_HEADS = 16
MLA_Q_RANK = 512
MLA_KV_RANK = 512
MLA_NOPE = 128
MLA_ROPE = 64
MLA_V = 128
MLA_QK = MLA_NOPE + MLA_ROPE
ODD_IN = MLA_Q_RANK + MLA_KV_RANK + MLA_ROPE
ATTN_BLOCK = 128
ROPE_THETA = 10000.0
MAX_OFFSET_CHUNKS = 64
D_FF = 5632
FFN_CONV = 3

kernel_name = "hybrid_gmlp_ssd_mla_convffn"


def rmsnorm(x, w):
    xf = x.astype(jnp.float32)
    y = xf * lax.rsqrt(jnp.mean(xf * xf, -1, keepdims=True) + EPS)
    return (y * w.astype(jnp.float32)).astype(x.dtype)


def causal_dwconv(x, w, b):
    k = w.shape[0]
    c = x.shape[-1]
    y = lax.conv_general_dilated(
        x, w[:, None, :].astype(x.dtype), window_strides=(1,),
        padding=[(k - 1, 0)], dimension_numbers=("NWC", "WIO", "NWC"),
        feature_group_count=c)
    return y + b.astype(x.dtype)


def gmlp_sgu(u, v, ln_g, ln_b, w_s, b_s):
    bsz, s, _ = u.shape
    nb = s // GM_BLOCK
    vf = v.reshape(bsz, nb, GM_BLOCK, GM_GROUPS, GM_GROUP_DIM).astype(jnp.float32)
    mu = jnp.mean(vf, -1, keepdims=True)
    var = jnp.mean(jnp.square(vf - mu), -1, keepdims=True)
    vn = ((vf - mu) * lax.rsqrt(var + EPS) * ln_g + ln_b).astype(u.dtype)
    chunk_id = jnp.arange(GM_BLOCK) // CHUNK
    mask = chunk_id[:, None] >= chunk_id[None, :]
    ws = jnp.where(mask, w_s, jnp.zeros((), w_s.dtype)).astype(u.dtype)
    gate = jnp.einsum("gij,bnjgc->bnigc", ws, vn) + b_s.T[:, :, None].astype(u.dtype)
    return u * gate.reshape(bsz, s, GM_WIDTH)


def segsum(a):
    t = a.shape[-1]
    cs = jnp.cumsum(a, -1)
    d = cs[..., :, None] - cs[..., None, :]
    mask = jnp.tril(jnp.ones((t, t), dtype=bool))
    return jnp.where(mask, d, -jnp.inf)


def ssd_scan(x, dt, a, b, c):
    bsz, s = x.shape[:2]
    nc = s // SSM_CHUNK
    G, E, P, N, L = SSM_GROUPS, SSM_HEADS_PER_GROUP, SSM_HEAD_DIM, SSM_STATE, SSM_CHUNK
    xd = (x * dt[..., None]).reshape(bsz, nc, L, G, E, P)
    da = jnp.moveaxis((dt * a).reshape(bsz, nc, L, G, E), 2, -1)
    bc = b.reshape(bsz, nc, L, G, N)
    cc = c.reshape(bsz, nc, L, G, N)
    a_cum = jnp.cumsum(da, -1)
    decay = jnp.exp(segsum(da))
    cb = jnp.einsum("bclgn,bcsgn->bcgls", cc, bc)
    y_diag = jnp.einsum("bcgls,bcgels,bcsgep->bclgep", cb, decay, xd)
    decay_states = jnp.exp(a_cum[..., -1:] - a_cum)
    states = jnp.einsum("bclgn,bcgel,bclgep->bcgepn", bc, decay_states, xd)
    chunk_decay = jnp.exp(a_cum[..., -1])

    def step(h, inp):
        s_c, d_c = inp
        return d_c[..., None, None] * h + s_c, h

    h0 = jnp.zeros((bsz, G, E, P, N), jnp.float32)
    _, prev = lax.scan(step, h0, (jnp.moveaxis(states, 1, 0), jnp.moveaxis(chunk_decay, 1, 0)))
    prev = jnp.moveaxis(prev, 0, 1)
    y_off = jnp.einsum("bclgn,bcgepn,bcgel->bclgep", cc, prev, jnp.exp(a_cum))
    return (y_diag + y_off).reshape(bsz, s, SSM_HEADS, P)


def mamba_branch(z, xbc, dt_raw, conv_w, conv_b, dt_bias, a_log, d_skip, norm_w):
    bsz, s, _ = z.shape
    xbc = jax.nn.silu(causal_dwconv(xbc, conv_w, conv_b)).astype(jnp.float32)
    xs, bs, cs = jnp.split(xbc, [SSM_D_INNER, SSM_D_INNER + SSM_GROUPS * SSM_STATE], axis=-1)
    dt = jax.nn.softplus(dt_raw.astype(jnp.float32) + dt_bias.astype(jnp.float32))
    a = -jnp.exp(a_log.astype(jnp.float32))
    xh = xs.reshape(bsz, s, SSM_HEADS, SSM_HEAD_DIM)
    y = ssd_scan(xh, dt, a,
                 bs.reshape(bsz, s, SSM_GROUPS, SSM_STATE),
                 cs.reshape(bsz, s, SSM_GROUPS, SSM_STATE))
    y = y + d_skip.astype(jnp.float32)[:, None] * xh
    y = y.reshape(bsz, s, SSM_D_INNER) * jax.nn.silu(z.astype(jnp.float32))
    y = y.reshape(bsz, s, SSM_GROUPS, SSM_D_INNER // SSM_GROUPS)
    y = y * lax.rsqrt(jnp.mean(y * y, -1, keepdims=True) + EPS)
    y = y.reshape(bsz, s, SSM_D_INNER) * norm_w.astype(jnp.float32)
    return y.astype(z.dtype)


def even_mixer(h, w_in, gm_ln_g, gm_ln_b, gm_ws, gm_bs, conv_w, conv_b,
               dt_bias, a_log, d_skip, ssm_norm_w, w_out):
    proj = h @ w_in
    o1 = GM_WIDTH
    o2 = 2 * GM_WIDTH
    o3 = o2 + SSM_D_INNER
    o4 = o3 + SSM_CONV_DIM
    u, v, z, xbc, dt_raw = jnp.split(proj, [o1, o2, o3, o4], axis=-1)
    ya = gmlp_sgu(jax.nn.gelu(u), jax.nn.gelu(v), gm_ln_g, gm_ln_b, gm_ws, gm_bs)
    yb = mamba_branch(z, xbc, dt_raw, conv_w, conv_b, dt_bias, a_log, d_skip, ssm_norm_w)
    return jnp.concatenate([ya, yb], axis=-1) @ w_out


def rope(x, cos, sin):
    half = x.shape[-1] // 2
    x1, x2 = x[..., :half], x[..., half:]
    return jnp.concatenate([x1 * cos - x2 * sin, x1 * sin + x2 * cos], axis=-1)


def mla_mixer(h, cos, sin, w_in, q_norm_w, kv_norm_w, w_uq, w_ukv, w_o):
    bsz, s, _ = h.shape
    proj = h @ w_in
    cq, ckv, kr = jnp.split(proj, [MLA_Q_RANK, MLA_Q_RANK + MLA_KV_RANK], axis=-1)
    cq = rmsnorm(cq, q_norm_w)
    ckv = rmsnorm(ckv, kv_norm_w)
    q = (cq @ w_uq).reshape(bsz, s, MLA_HEADS, MLA_QK)
    q_nope = q[..., :MLA_NOPE]
    q_pe = rope(q[..., MLA_NOPE:], cos[:, :, None, :], sin[:, :, None, :])
    kv = (ckv @ w_ukv).reshape(bsz, s, MLA_HEADS, MLA_NOPE + MLA_V)
    k_nope, v = kv[..., :MLA_NOPE], kv[..., MLA_NOPE:]
    k_pe = rope(kr, cos, sin)
    scale = MLA_QK ** -0.5
    outs = []
    for i in range(s // ATTN_BLOCK):
        q0 = i * ATTN_BLOCK
        kend = q0 + ATTN_BLOCK
        sc = (jnp.einsum("bqhd,bkhd->bhqk", q_nope[:, q0:kend], k_nope[:, :kend])
              + jnp.einsum("bqhr,bkr->bhqk", q_pe[:, q0:kend], k_pe[:, :kend]))
        sc = sc.astype(jnp.float32) * scale
        qc = (q0 + jnp.arange(ATTN_BLOCK)) // CHUNK
        kc = jnp.arange(kend) // CHUNK
        sc = jnp.where(kc[None, :] <= qc[:, None], sc, -jnp.inf)
        p = jax.nn.softmax(sc, axis=-1).astype(v.dtype)
        outs.append(jnp.einsum("bhqk,bkhd->bqhd", p, v[:, :kend]))
    o = jnp.concatenate(outs, axis=1).reshape(bsz, s, MLA_HEADS * MLA_V)
    return o @ w_o


def conv_ffn(h, w_up, conv_w, conv_b, w_down):
    up = h @ w_up
    g, val = up[..., :D_FF], up[..., D_FF:]
    g = causal_dwconv(g, conv_w, conv_b)
    return (jax.nn.gelu(g) * val) @ w_down


def setup_inputs(seed: int = 0) -> dict:
    key = jax.random.key(seed)
    ks = iter(jax.random.split(key, 40))

    def nrm(shape, scale):
        return jax.random.normal(next(ks), shape, jnp.float32) * scale

    def gain(shape):
        return 1.0 + nrm(shape, 0.02)

    x = jax.random.normal(next(ks), (BATCH, SEQ, D_MODEL), jnp.float32)
    offset = jax.random.randint(next(ks), (BATCH,), 0, MAX_OFFSET_CHUNKS) * CHUNK
    positions = (offset[:, None] + jnp.arange(SEQ)[None, :]).astype(jnp.int32)

    dt = jnp.exp(jax.random.uniform(next(ks), (N_EVEN, SSM_HEADS), jnp.float32)
                 * (math.log(0.1) - math.log(0.001)) + math.log(0.001))
    dt_bias = dt + jnp.log(-jnp.expm1(-dt))
    a_log = jnp.log(jax.random.uniform(next(ks), (N_EVEN, SSM_HEADS), jnp.float32, 1.0, 16.0))

    return {
        "x": x,
        "positions": positions,
        "norm_mix": gain((DEPTH, D_MODEL)),
        "norm_ffn": gain((DEPTH, D_MODEL)),
        "norm_final": gain((D_MODEL,)),
        "ev_w_in": nrm((N_EVEN, D_MODEL, EVEN_IN), D_MODEL ** -0.5),
        "ev_gm_ln_g": gain((N_EVEN, GM_GROUPS, GM_GROUP_DIM)),
        "ev_gm_ln_b": nrm((N_EVEN, GM_GROUPS, GM_GROUP_DIM), 0.02),
        "ev_gm_ws": nrm((N_EVEN, GM_GROUPS, GM_BLOCK, GM_BLOCK), GM_BLOCK ** -0.5),
        "ev_gm_bs": gain((N_EVEN, GM_GROUPS, GM_BLOCK)),
        "ev_conv_w": nrm((N_EVEN, SSM_CONV, SSM_CONV_DIM), SSM_CONV ** -0.5),
        "ev_conv_b": nrm((N_EVEN, SSM_CONV_DIM), 0.02),
        "ev_dt_bias": dt_bias,
        "ev_a_log": a_log,
        "ev_d_skip": gain((N_EVEN, SSM_HEADS)),
        "ev_ssm_norm_w": gain((N_EVEN, SSM_D_INNER)),
        "ev_w_out": nrm((N_EVEN, EVEN_MIX, D_MODEL), EVEN_MIX ** -0.5),
        "od_w_in": nrm((N_ODD, D_MODEL, ODD_IN), D_MODEL ** -0.5),
        "od_q_norm": gain((N_ODD, MLA_Q_RANK)),
        "od_kv_norm": gain((N_ODD, MLA_KV_RANK)),
        "od_w_uq": nrm((N_ODD, MLA_Q_RANK, MLA_HEADS * MLA_QK), MLA_Q_RANK ** -0.5),
        "od_w_ukv": nrm((N_ODD, MLA_KV_RANK, MLA_HEADS * (MLA_NOPE + MLA_V)), MLA_KV_RANK ** -0.5),
        "od_w_o": nrm((N_ODD, MLA_HEADS * MLA_V, D_MODEL), (MLA_HEADS * MLA_V) ** -0.5),
        "ff_w_up": nrm((DEPTH, D_MODEL, 2 * D_FF), D_MODEL ** -0.5),
        "ff_conv_w": nrm((DEPTH, FFN_CONV, D_FF), FFN_CONV ** -0.5),
        "ff_conv_b": nrm((DEPTH, D_FF), 0.02),
        "ff_w_down": nrm((DEPTH, D_FF, D_MODEL), D_FF ** -0.5),
    }


def reference(x, positions, norm_mix, norm_ffn, norm_final,
              ev_w_in, ev_gm_ln_g, ev_gm_ln_b, ev_gm_ws, ev_gm_bs,
              ev_conv_w, ev_conv_b, ev_dt_bias, ev_a_log, ev_d_skip, ev_ssm_norm_w, ev_w_out,
              od_w_in, od_q_norm, od_kv_norm, od_w_uq, od_w_ukv, od_w_o,
              ff_w_up, ff_conv_w, ff_conv_b, ff_w_down):
    inv_freq = ROPE_THETA ** (-jnp.arange(0, MLA_ROPE, 2, dtype=jnp.float32) / MLA_ROPE)
    ang = positions.astype(jnp.float32)[..., None] * inv_freq
    cos = jnp.cos(ang).astype(x.dtype)
    sin = jnp.sin(ang).astype(x.dtype)

    h = x
    for layer in range(DEPTH):
        j = layer // 2
        hn = rmsnorm(h, norm_mix[layer])
        if layer % 2 == 0:
            mix = even_mixer(hn, ev_w_in[j], ev_gm_ln_g[j], ev_gm_ln_b[j], ev_gm_ws[j], ev_gm_bs[j],
                             ev_conv_w[j], ev_conv_b[j], ev_dt_bias[j], ev_a_log[j], ev_d_skip[j],
                             ev_ssm_norm_w[j], ev_w_out[j])
        else:
            mix = mla_mixer(hn, cos, sin, od_w_in[j], od_q_norm[j], od_kv_norm[j],
                            od_w_uq[j], od_w_ukv[j], od_w_o[j])
        h = h + mix
        h = h + conv_ffn(rmsnorm(h, norm_ffn[layer]), ff_w_up[layer], ff_conv_w[layer],
                         ff_conv_b[layer], ff_w_down[layer])
    return rmsnorm(h, norm_final)
```

```python
import numpy as np
import ml_dtypes
import concourse.bass as bass
import concourse.mybir as mybir
from concourse.bass_utils import run_bass_kernel_spmd

F32 = mybir.dt.float32
BF16 = mybir.dt.bfloat16
I32 = mybir.dt.int32
AF = mybir.ActivationFunctionType
ALU = mybir.AluOpType
BF = ml_dtypes.bfloat16

D = 2048
DFF = 5632
SEQ = 2048
EPS = 1e-6
NKC = D // 128
NFC = DFF // 128


class Ticket:
    __slots__ = ("sem", "val")

    def __init__(self, sem, val):
        self.sem = sem
        self.val = val


class Tok:
    __slots__ = ("w", "r")

    def __init__(self):
        self.w = None
        self.r = {}

    def add_read(self, tk):
        k = id(tk.sem)
        o = self.r.get(k)
        if o is None or o.val < tk.val:
            self.r[k] = tk

    def set_write(self, tk):
        self.w = tk
        self.r = {}


class Q:
    def __init__(self, prog, eng, name):
        self.prog = prog
        self.eng = eng
        self.name = name
        self.sem = prog.new_sem("q_" + name)
        self.count = 0
        self.seen = {}
        self.pending = []
        self.dma_sems = []
        self.dma_i = 0

    def wait(self, tickets):
        for t in tickets:
            if t is None:
                continue
            k = id(t.sem)
            if self.seen.get(k, 0) < t.val:
                self.eng.wait_ge(t.sem, t.val)
                self.seen[k] = t.val

    @staticmethod
    def _deps(reads, writes):
        d = []
        for t in reads:
            d.append(t.w)
        for t in writes:
            d.append(t.w)
            d.extend(t.r.values())
        return d

    def do(self, fn, reads=(), writes=(), inc=True):
        self.wait(self._deps(reads, writes))
        ins = fn(self.eng)
        self.pending.append((reads, writes))
        if inc:
            self.count += 1
            ins.then_inc(self.sem, 1)
            tk = Ticket(self.sem, self.count)
            for (rs, ws) in self.pending:
                for t in rs:
                    t.add_read(tk)
                for t in ws:
                    t.set_write(tk)
            self.pending = []
        return ins

    def dma(self, out, in_, reads=(), writes=()):
        self.wait(self._deps(reads, writes))
        if not self.dma_sems:
            self.dma_sems = [[self.prog.new_sem("d_%s%d" % (self.name, i)), 0]
                             for i in range(self.prog.n_dma_sems)]
        slot = self.dma_sems[self.dma_i % len(self.dma_sems)]
        self.dma_i += 1
        sem, cnt = slot
        if cnt > 0:
            self.wait([Ticket(sem, cnt)])
        ins = self.eng.dma_start(out=out, in_=in_)
        slot[1] = cnt + 16
        ins.then_inc(sem, 16)
        tk = Ticket(sem, cnt + 16)
        for t in reads:
            t.add_read(tk)
        for t in writes:
            t.set_write(tk)
        return tk


class Prog:
    def __init__(self, n_dma_sems=14):
        self.nc = bass.Bass("TRN2", target_bir_lowering=False)
        self.n_dma_sems = n_dma_sems
        self._stack = []
        nc = self.nc
        self.pe = Q(self, nc.tensor, "pe")
        self.act = Q(self, nc.scalar, "act")
        self.dve = Q(self, nc.vector, "dve")
        self.pool = Q(self, nc.gpsimd, "pool")
        self.sp = Q(self, nc.sync, "sp")
        self.queues = [self.pe, self.act, self.dve, self.pool, self.sp]
        self._names = 0

    def new_sem(self, name):
        cm = self.nc.semaphore(name)
        s = cm.__enter__()
        self._stack.append(cm)
        return s

    def _nm(self, name):
        self._names += 1
        return "%s_%d" % (name, self._names)

    def mark(self):
        return len(self._stack)

    def release(self, mark):
        self.barrier()
        while len(self._stack) > mark:
            self._stack.pop().__exit__(None, None, None)

    def sbuf(self, name, shape, dtype):
        cm = self.nc.sbuf_tensor(self._nm(name), shape, dtype)
        v = cm.__enter__()
        self._stack.append(cm)
        return v

    def psum(self, name, shape, dtype=F32):
        cm = self.nc.psum_tensor(self._nm(name), shape, dtype)
        v = cm.__enter__()
        self._stack.append(cm)
        return v

    def dram(self, name, shape, dtype, kind="Internal"):
        return self.nc.dram_tensor(name, shape, dtype, kind=kind).ap()

    def all_tickets(self):
        tks = []
        for q in self.queues:
            if q.count:
                tks.append(Ticket(q.sem, q.count))
            for sem, cnt in q.dma_sems:
                if cnt:
                    tks.append(Ticket(sem, cnt))
        return tks

    def barrier(self):
        tks = self.all_tickets()
        for q in self.queues:
            q.wait(tks)

    def finish(self):
        self.barrier()
        while self._stack:
            self._stack.pop().__exit__(None, None, None)
        return self.nc


class Ring:
    def __init__(self, items):
        self.items = items
        self.i = 0

    def next(self):
        it = self.items[self.i % len(self.items)]
        self.i += 1
        return it


def psum_ring(P, nbanks, name="ps"):
    t = P.psum(name, [128, nbanks, 512], F32)
    return t, Ring([(t[:, b, :], Tok()) for b in range(nbanks)])


def gemm_fm(P, w_dram, kc, panels, wring, x_of, x_toks, tiles, psring, evac, prefetch=1):
    wsrc = w_dram.rearrange("(c p) m -> p c m", p=128)
    loaded = {}

    def load(pi):
        if pi >= len(panels) or pi in loaded:
            return
        col0, ncols = panels[pi]
        wt, wtok = wring.next()
        P.pool.dma(wt[:, :, 0:ncols], wsrc[:, :, col0:col0 + ncols], writes=[wtok])
        loaded[pi] = (wt, wtok)

    for pi in range(min(prefetch, len(panels))):
        load(pi)
    mi = 0
    for pi, (col0, ncols) in enumerate(panels):
        load(pi + prefetch)
        wt, wtok = loaded.pop(pi)
        for mo in range(0, ncols, 128):
            mw = min(128, ncols - mo)
            outs = []
            for (c0, n) in tiles:
                ps, pstok = psring.next()
                for k in range(kc):
                    P.pe.do(lambda e: e.matmul(ps[0:mw, 0:n], wt[:, k, mo:mo + mw], x_of(k, c0, n),
                                               start=(k == 0), stop=(k == kc - 1)),
                            reads=[wtok] + list(x_toks), writes=[pstok], inc=(k == kc - 1))
                outs.append((ps, pstok, c0, n))
            evac(mi, col0 + mo, mw, outs)
            mi += 1


def ph_norm(P, hT, tiles, gain_sb, gain_tok, ones_bf, ones_tok, psring,
            hn_sb=None, hn_tok=None, out_dram=None, out_tiles=None, out_dtype=BF16):
    mk = P.mark()
    hsrc = hT.rearrange("(c p) t -> p c t", p=128)
    hb = [(P.sbuf("nrm_h", [128, NKC, 512], F32), Tok()) for _ in range(2)]
    sq = [(P.sbuf("nrm_sq", [128, 512], BF16), Tok()) for _ in range(3)]
    rs = [(P.sbuf("nrm_rs", [128, 512], F32), Tok()) for _ in range(2)]
    ob = None
    if hn_sb is None:
        ob = [(P.sbuf("nrm_o", [128, NKC, 512], out_dtype), Tok()) for _ in range(2)]
    odst = out_dram.rearrange("(c p) t -> p c t", p=128) if out_dram is not None else None
    run = 0
    for ti, (c0, n) in enumerate(tiles):
        h, htok = hb[ti % 2]
        for c4 in range(0, NKC, 4):
            P.sp.dma(h[:, c4:c4 + 4, 0:n], hsrc[:, c4:c4 + 4, c0:c0 + n], writes=[htok])
        ps, pstok = psring.next()
        for c in range(NKC):
            s, stok = sq[c % 3]
            P.act.do(lambda e: e.activation(out=s[:, 0:n], in_=h[:, c, 0:n], func=AF.Square), reads=[htok], writes=[stok])
            P.pe.do(lambda e: e.matmul(ps[:, 0:n], ones_bf[:, :], s[:, 0:n], start=(c == 0), stop=(c == NKC - 1)),
                    reads=[stok, ones_tok], writes=[pstok], inc=True)
        r, rtok = rs[ti % 2]
        P.act.do(lambda e: e.activation(out=r[:, 0:n], in_=ps[:, 0:n], func=AF.Sqrt, scale=1.0 / D, bias=EPS),
                 reads=[pstok], writes=[rtok])
        P.dve.do(lambda e: e.reciprocal(out=r[:, 0:n], in_=r[:, 0:n]), reads=[rtok], writes=[rtok])
        if ob is not None:
            o, otok = ob[ti % 2]
        for c in range(NKC):
            if ob is not None:
                dst = o[:, c, 0:n]
                wtoks = [otok]
            else:
                dst = hn_sb[:, c, run:run + n]
                wtoks = [hn_tok]
            P.dve.do(lambda e: e.scalar_tensor_tensor(out=dst, in0=h[:, c, 0:n], scalar=gain_sb[:, c:c + 1], in1=r[:, 0:n],
                                                      op0=ALU.mult, op1=ALU.mult),
                     reads=[htok, rtok, gain_tok], writes=wtoks)
        if odst is not None:
            oc = out_tiles[ti]
            if ob is not None:
                P.sp.dma(odst[:, :, oc:oc + n], o[:, :, 0:n], reads=[otok])
            else:
                P.sp.dma(odst[:, :, oc:oc + n], hn_sb[:, :, run:run + n], reads=[hn_tok])
        run += n
    P.release(mk)


def wring_make(P, n, kc, mp, name="wr"):
    return Ring([(P.sbuf(name, [128, kc, mp], BF16), Tok()) for _ in range(n)])


def ph_gmlp(P, hn_sb, hn_tok, w_uv, ln_g, ln_b, wsT, maskT, bs, ya_out, out_c0, psring, consts):
    NT = 1024
    NB = NT // 128
    mk = P.mark()
    wring = wring_make(P, 2, NKC, 512, "gm_w")
    lng = P.sbuf("gm_lng", [128, 2048], F32); lng_t = Tok()
    lnb = P.sbuf("gm_lnb", [128, 2048], F32); lnb_t = Tok()
    bsb = P.sbuf("gm_bsb", [128, 8, 128], F32); bsb_t = Tok()
    wm32 = P.sbuf("gm_wm32", [128, 8, 128], F32); wm32_t = Tok()
    msk = P.sbuf("gm_msk", [128, 128], F32); msk_t = Tok()
    WM = P.sbuf("gm_wm", [128, 8, 128], BF16); wm_t = Tok()
    P.sp.dma(lng[:], ln_g.partition_broadcast(128), writes=[lng_t])
    P.sp.dma(lnb[:], ln_b.partition_broadcast(128), writes=[lnb_t])
    P.sp.dma(bsb[:].rearrange("p g i -> p (g i)"), bs.partition_broadcast(128), writes=[bsb_t])
    P.sp.dma(wm32[:], wsT, writes=[wm32_t])
    P.sp.dma(msk[:], maskT, writes=[msk_t])
    P.dve.do(lambda e: e.tensor_tensor(out=WM[:], in0=wm32[:], in1=msk[:].unsqueeze(1).to_broadcast([128, 8, 128]), op=ALU.mult),
             reads=[wm32_t, msk_t], writes=[wm_t])
    U = P.sbuf("gm_u", [128, 8, NT], BF16)
    V = P.sbuf("gm_v", [128, NB, 1024], F32)
    st = P.sbuf("gm_st", [128, 4, 6], F32); st_t = Tok()
    mv = P.sbuf("gm_mv", [128, 4, 2], F32); mv_t = Tok()
    rstd = P.sbuf("gm_rstd", [128, 4], F32); rstd_t = Tok()
    t1 = [(P.sbuf("gm_t1", [128, 1024], F32), Tok()) for _ in range(2)]
    vn = [(P.sbuf("gm_vn", [128, 1024], BF16), Tok()) for _ in range(2)]
    tb2 = [(P.sbuf("gm_tb", [128, 2, 128], F32), Tok()) for _ in range(2)]
    YA = P.sbuf("gm_ya", [128, 8, NT], BF16)
    ydst = ya_out.rearrange("(c p) t -> p c t", p=128)
    u_t = [Tok() for _ in range(8)]
    v_t = [Tok() for _ in range(NB)]
    ya_t = Tok()
    for half in range(2):

        def evac_u(mi, col0, mw, outs):
            for (ps, pstok, c0, n) in outs:
                P.act.do(lambda e: e.activation(out=U[:, mi, c0:c0 + n], in_=ps[:, 0:n], func=AF.Gelu_apprx_tanh),
                         reads=[pstok], writes=[u_t[mi]])

        gemm_fm(P, w_uv, NKC, [(half * 1024, 512), (half * 1024 + 512, 512)], wring,
                lambda k, c0, n: hn_sb[:, k, c0:c0 + n], [hn_tok], [(0, 512), (512, 512)], psring, evac_u)
        wsrc = w_uv.rearrange("(c p) m -> p c m", p=128)
        for cp in range(2):
            wt, wtok = wring.next()
            col0 = 2048 + half * 1024 + cp * 512
            P.pool.dma(wt[:, :, 0:512], wsrc[:, :, col0:col0 + 512], writes=[wtok])
            for tb in range(NB):
                ps, pstok = psring.next()
                for k in range(NKC):
                    P.pe.do(lambda e: e.matmul(ps[:, :], hn_sb[:, k, tb * 128:(tb + 1) * 128], wt[:, k, :],
                                               start=(k == 0), stop=(k == NKC - 1)),
                            reads=[wtok, hn_tok], writes=[pstok], inc=(k == NKC - 1))
                P.act.do(lambda e: e.activation(out=V[:, tb, cp * 512:(cp + 1) * 512], in_=ps[:, :], func=AF.Gelu_apprx_tanh),
                         reads=[pstok], writes=[v_t[tb]])
        for tb in range(NB):
            for g in range(4):
                P.dve.do(lambda e: e.bn_stats(out=st[:, g, :], in_=V[:, tb, g * 256:(g + 1) * 256]), reads=[v_t[tb]], writes=[st_t])
            for g in range(4):
                P.dve.do(lambda e: e.bn_aggr(out=mv[:, g, :], in_=st[:, g, :]), reads=[st_t], writes=[mv_t])
            P.dve.do(lambda e: e.tensor_scalar(out=rstd[:], in0=mv[:, :, 1], scalar1=EPS, scalar2=None, op0=ALU.add),
                     reads=[mv_t], writes=[rstd_t])
            P.act.do(lambda e: e.activation(out=rstd[:], in_=rstd[:], func=AF.Sqrt), reads=[rstd_t], writes=[rstd_t])
            P.dve.do(lambda e: e.reciprocal(out=rstd[:], in_=rstd[:]), reads=[rstd_t], writes=[rstd_t])
            t, t_t = t1[tb % 2]
            for g in range(4):
                P.dve.do(lambda e: e.tensor_scalar(out=t[:, g * 256:(g + 1) * 256], in0=V[:, tb, g * 256:(g + 1) * 256],
                                                   scalar1=mv[:, g, 0:1], scalar2=rstd[:, g:g + 1],
                                                   op0=ALU.subtract, op1=ALU.mult),
                         reads=[v_t[tb], mv_t, rstd_t], writes=[t_t])
            P.dve.do(lambda e: e.tensor_tensor(out=t[:], in0=t[:], in1=lng[:, half * 1024:(half + 1) * 1024], op=ALU.mult),
                     reads=[t_t, lng_t], writes=[t_t])
            v_, vn_t = vn[tb % 2]
            P.dve.do(lambda e: e.tensor_tensor(out=v_[:], in0=t[:], in1=lnb[:, half * 1024:(half + 1) * 1024], op=ALU.add),
                     reads=[t_t, lnb_t], writes=[vn_t])
            for bank in range(2):
                ps, pstok = psring.next()
                for q in range(4):
                    m = bank * 4 + q
                    g = half * 4 + m // 2
                    P.pe.do(lambda e: e.matmul(ps[:, q * 128:(q + 1) * 128], v_[:, m * 128:(m + 1) * 128], WM[:, g, :],
                                               start=True, stop=True),
                            reads=[vn_t, wm_t], writes=[pstok], inc=(q == 3))
                for pr in range(2):
                    m = bank * 4 + pr * 2
                    g = half * 4 + m // 2
                    tt_, tt_t = tb2[pr]
                    P.dve.do(lambda e: e.tensor_tensor(out=tt_[:], in0=ps[:, pr * 256:(pr + 1) * 256].rearrange("p (a i) -> p a i", a=2),
                                                       in1=bsb[:, g, :].unsqueeze(1).to_broadcast([128, 2, 128]), op=ALU.add),
                             reads=[pstok, bsb_t], writes=[tt_t])
                    P.dve.do(lambda e: e.tensor_tensor(out=YA[:, m:m + 2, tb * 128:(tb + 1) * 128], in0=tt_[:],
                                                       in1=U[:, m:m + 2, tb * 128:(tb + 1) * 128], op=ALU.mult),
                             reads=[tt_t, u_t[m], u_t[m + 1]], writes=[ya_t])
        P.sp.dma(ydst[:, half * 8:(half + 1) * 8, out_c0:out_c0 + NT], YA[:], reads=[ya_t])
    P.release(mk)


def ph_t2(P, y_srcs, y_c0, hT, h_c0, flag_sb, flag_tok, w_out, g_ffn_sb, g_ffn_tok, w_up, cw_d, cb_d, w_down,
          h_out, out_c0, psring, consts):
    NT = 1024
    W = NT + 2
    ones_bf, ones_tok = consts["ones"]
    hmid_halo = consts["hmid_halo"]
    TIL = [(0, 2), (2, 512), (514, 512)]
    mk0 = P.mark()
    hn2 = P.sbuf("t2_hn2", [128, NKC, W], BF16); hn2_t = Tok()
    cw = P.sbuf("t2_cw", [128, NFC, 3], F32); cw_t = Tok()
    cb = P.sbuf("t2_cb", [128, NFC], F32); cb_t = Tok()
    P.sp.dma(cw[:], cw_d, writes=[cw_t])
    P.sp.dma(cb[:], cb_d, writes=[cb_t])
    hsrc = hT.rearrange("(c p) t -> p c t", p=128)
    hdst = h_out.rearrange("(c p) t -> p c t", p=128)
    hhdst = hmid_halo.rearrange("(c p) t -> p c t", p=128)
    hmid_t = Tok()
    hh_t = Tok()
    for si, ys in enumerate(y_srcs):
        mk = P.mark()
        Y = P.sbuf("t2_y", [128, NKC, W], BF16); y_t = Tok()
        ysrc = ys.rearrange("(c p) t -> p c t", p=128)
        for c4 in range(0, NKC, 4):
            P.sp.dma(Y[:, c4:c4 + 4, :], ysrc[:, c4:c4 + 4, y_c0:y_c0 + W], writes=[y_t])
        wring = wring_make(P, 2, NKC, 512, "t2_wo")
        hin = [(P.sbuf("t2_hin", [128, W], F32), Tok()) for _ in range(3)]
        hmo = [(P.sbuf("t2_hmo", [128, W], F32), Tok()) for _ in range(3)]

        def load_h(m):
            b, bt = hin[m % 3]
            if si == 0:
                P.sp.dma(b[:], hsrc[:, m, h_c0:h_c0 + W], writes=[bt])
            else:
                P.sp.dma(b[:, 0:2], hhdst[:, m, :], reads=[hh_t], writes=[bt])
                P.sp.dma(b[:, 2:W], hdst[:, m, out_c0:out_c0 + NT], reads=[hmid_t], writes=[bt])

        load_h(0)
        load_h(1)

        def evac_o(mi, col0, mw, outs):
            b, bt = hin[mi % 3]
            o, ot = hmo[mi % 3]
            for (ps, pstok, c0, n) in outs:
                P.dve.do(lambda e: e.tensor_tensor(out=o[:, c0:c0 + n], in0=ps[:, 0:n], in1=b[:, c0:c0 + n], op=ALU.add),
                         reads=[pstok, bt], writes=[ot])
            P.sp.dma(hdst[:, mi, out_c0:out_c0 + NT], o[:, 2:W], reads=[ot], writes=[hmid_t])
            P.sp.dma(hhdst[:, mi, :], o[:, 0:2], reads=[ot], writes=[hh_t])
            if mi + 2 < NKC:
                load_h(mi + 2)

        gemm_fm(P, w_out[si * 2048:(si + 1) * 2048, :], NKC, [(i * 512, 512) for i in range(4)], wring,
                lambda k, c0, n: Y[:, k, c0:c0 + n], [y_t], TIL, psring, evac_o)
        P.release(mk)
    ph_norm_multi(P, [(hmid_halo, 0, 2), (h_out, out_c0, 512), (h_out, out_c0 + 512, 512)],
                  g_ffn_sb, g_ffn_tok, ones_bf, ones_tok, psring, hn2, hn2_t, [hmid_t, hh_t])
    wsrc = w_up.rearrange("(c p) m -> p c m", p=128)
    HF = NFC // 2
    for fh in range(2):
        mkh = P.mark()
        HID = P.sbuf("t2_hid", [128, HF, NT], BF16)
        hid_t = [Tok() for _ in range(HF)]
        mk = P.mark()
        wring = wring_make(P, 2, NKC, 512, "t2_wu")
        G = [(P.sbuf("t2_g", [128, W], F32), Tok()) for _ in range(2)]
        A = [(P.sbuf("t2_a", [128, NT], F32), Tok()) for _ in range(2)]
        GA = [(P.sbuf("t2_ga", [128, NT], F32), Tok()) for _ in range(2)]
        NP = HF // 2

        def load_up(pi):
            wt, wtok = wring.next()
            cg = (fh * HF + pi * 2) * 128
            P.pool.dma(wt[:, :, 0:256], wsrc[:, :, cg:cg + 256], writes=[wtok])
            P.pool.dma(wt[:, :, 256:512], wsrc[:, :, DFF + cg:DFF + cg + 256], writes=[wtok])
            return wt, wtok

        nxt = load_up(0)
        for pi in range(NP):
            wt, wtok = nxt
            if pi + 1 < NP:
                nxt = load_up(pi + 1)
            for jj in range(2):
                jl = pi * 2 + jj
                j = fh * HF + jl
                gps = []
                for (c0, n) in TIL:
                    ps, pstok = psring.next()
                    for k in range(NKC):
                        P.pe.do(lambda e: e.matmul(ps[:, 0:n], wt[:, k, jj * 128:(jj + 1) * 128], hn2[:, k, c0:c0 + n],
                                                   start=(k == 0), stop=(k == NKC - 1)),
                                reads=[wtok, hn2_t], writes=[pstok], inc=(k == NKC - 1))
                    gps.append((ps, pstok, c0, n))
                vps = []
                for (c0, n) in TIL[1:]:
                    ps, pstok = psring.next()
                    for k in range(NKC):
                        P.pe.do(lambda e: e.matmul(ps[:, 0:n], wt[:, k, 256 + jj * 128:256 + (jj + 1) * 128], hn2[:, k, c0:c0 + n],
                                                   start=(k == 0), stop=(k == NKC - 1)),
                                reads=[wtok, hn2_t], writes=[pstok], inc=(k == NKC - 1))
                    vps.append((ps, pstok, c0, n))
                g, g_t = G[jl % 2]
                a, a_t = A[jl % 2]
                ga, ga_t = GA[jl % 2]
                ps, pstok, c0, n = gps[0]
                P.dve.do(lambda e: e.tensor_scalar(out=g[:, 0:2], in0=ps[:, 0:2], scalar1=flag_sb[:, 0:1], scalar2=None, op0=ALU.mult),
                         reads=[pstok, flag_tok], writes=[g_t])
                for (ps, pstok, c0, n) in gps[1:]:
                    P.act.do(lambda e: e.activation(out=g[:, c0:c0 + n], in_=ps[:, 0:n], func=AF.Copy), reads=[pstok], writes=[g_t])
                P.dve.do(lambda e: e.tensor_scalar(out=a[:], in0=g[:, 2:W], scalar1=cw[:, j, 2:3], scalar2=cb[:, j:j + 1],
                                                   op0=ALU.mult, op1=ALU.add), reads=[g_t, cw_t, cb_t], writes=[a_t])
                P.dve.do(lambda e: e.scalar_tensor_tensor(out=a[:], in0=g[:, 1:W - 1], scalar=cw[:, j, 1:2], in1=a[:],
                                                          op0=ALU.mult, op1=ALU.add), reads=[g_t, cw_t, a_t], writes=[a_t])
                P.dve.do(lambda e: e.scalar_tensor_tensor(out=a[:], in0=g[:, 0:W - 2], scalar=cw[:, j, 0:1], in1=a[:],
                                                          op0=ALU.mult, op1=ALU.add), reads=[g_t, cw_t, a_t], writes=[a_t])
                P.act.do(lambda e: e.activation(out=ga[:], in_=a[:], func=AF.Gelu_apprx_tanh), reads=[a_t], writes=[ga_t])
                for (ps, pstok, c0, n) in vps:
                    P.dve.do(lambda e: e.tensor_tensor(out=HID[:, jl, c0 - 2:c0 - 2 + n], in0=ga[:, c0 - 2:c0 - 2 + n], in1=ps[:, 0:n],
                                                       op=ALU.mult), reads=[ga_t, pstok], writes=[hid_t[jl]])
        P.release(mk)
        mk = P.mark()
        wring2 = wring_make(P, 2, HF, 256, "t2_wd")
        hin = [(P.sbuf("t2_hin2", [128, NT], F32), Tok()) for _ in range(3)]
        hmo = [(P.sbuf("t2_hmo2", [128, NT], F32), Tok()) for _ in range(3)]

        def load_h2(m):
            b, bt = hin[m % 3]
            P.sp.dma(b[:], hdst[:, m, out_c0:out_c0 + NT], reads=[hmid_t], writes=[bt])

        load_h2(0)
        load_h2(1)

        def evac_d(mi, col0, mw, outs):
            b, bt = hin[mi % 3]
            o, ot = hmo[mi % 3]
            for (ps, pstok, c0, n) in outs:
                P.dve.do(lambda e: e.tensor_tensor(out=o[:, c0:c0 + n], in0=ps[:, 0:n], in1=b[:, c0:c0 + n], op=ALU.add),
                         reads=[pstok, bt], writes=[ot])
            P.sp.dma(hdst[:, mi, out_c0:out_c0 + NT], o[:], reads=[ot], writes=[hmid_t])
            if mi + 2 < NKC:
                load_h2(mi + 2)

        gemm_fm(P, w_down[fh * HF * 128:(fh + 1) * HF * 128, :], HF, [(i * 256, 256) for i in range(8)], wring2,
                lambda k, c0, n: HID[:, k, c0:c0 + n], hid_t, [(0, 512), (512, 512)], psring, evac_d)
        P.release(mk)
        P.release(mkh)
    P.release(mk0)


def ph_norm_multi(P, src_tiles, gain_sb, gain_tok, ones_bf, ones_tok, psring, hn_sb, hn_tok, src_toks):
    mk = P.mark()
    hb = [(P.sbuf("nrm_h", [128, NKC, 512], F32), Tok()) for _ in range(2)]
    sq = [(P.sbuf("nrm_sq", [128, 512], BF16), Tok()) for _ in range(3)]
    rs = [(P.sbuf("nrm_rs", [128, 512], F32), Tok()) for _ in range(2)]
    run = 0
    for ti, (src, c0, n) in enumerate(src_tiles):
        hsrc = src.rearrange("(c p) t -> p c t", p=128)
        h, htok = hb[ti % 2]
        for c4 in range(0, NKC, 4):
            P.sp.dma(h[:, c4:c4 + 4, 0:n], hsrc[:, c4:c4 + 4, c0:c0 + n], reads=src_toks, writes=[htok])
        ps, pstok = psring.next()
        for c in range(NKC):
            s, stok = sq[c % 3]
            P.act.do(lambda e: e.activation(out=s[:, 0:n], in_=h[:, c, 0:n], func=AF.Square), reads=[htok], writes=[stok])
            P.pe.do(lambda e: e.matmul(ps[:, 0:n], ones_bf[:, :], s[:, 0:n], start=(c == 0), stop=(c == NKC - 1)),
                    reads=[stok, ones_tok], writes=[pstok], inc=True)
        r, rtok = rs[ti % 2]
        P.act.do(lambda e: e.activation(out=r[:, 0:n], in_=ps[:, 0:n], func=AF.Sqrt, scale=1.0 / D, bias=EPS),
                 reads=[pstok], writes=[rtok])
        P.dve.do(lambda e: e.reciprocal(out=r[:, 0:n], in_=r[:, 0:n]), reads=[rtok], writes=[rtok])
        for c in range(NKC):
            P.dve.do(lambda e: e.scalar_tensor_tensor(out=hn_sb[:, c, run:run + n], in0=h[:, c, 0:n], scalar=gain_sb[:, c:c + 1],
                                                      in1=r[:, 0:n], op0=ALU.mult, op1=ALU.mult),
                     reads=[htok, rtok, gain_tok], writes=[hn_tok])
        run += n
    P.release(mk)


def load_small(P, dram_ap, shape, dtype=F32, name="sm", q=None):
    t = P.sbuf(name, shape, dtype)
    tok = Tok()
    (q or P.sp).dma(t[:], dram_ap, writes=[tok])
    return t, tok


def make_consts(P):
    ones = P.sbuf("c_ones", [128, 128], BF16)
    ones_t = Tok()
    P.dve.do(lambda e: e.memset(ones[:], 1.0), writes=[ones_t])
    return {"ones": (ones, ones_t)}


def build_A(with_gmlp=True):
    P = Prog()
    EI, EO = "ExternalInput", "ExternalOutput"
    hT = P.dram("hT", [D, 1024], F32, EI)
    gain = P.dram("gain", [128, NKC], F32, EI)
    hn_out = P.dram("hnT", [D, 1024], BF16, EO)
    consts = make_consts(P)
    _, psring = psum_ring(P, 8)
    g_sb, g_t = load_small(P, gain, [128, NKC], name="gain")
    hn = P.sbuf("hn", [128, NKC, 1024], BF16)
    hn_t = Tok()
    ph_norm(P, hT, [(0, 512), (512, 512)], g_sb, g_t, consts["ones"][0], consts["ones"][1], psring,
            hn_sb=hn, hn_tok=hn_t, out_dram=hn_out, out_tiles=[0, 512])
    if with_gmlp:
        w_uv = P.dram("w_uv", [D, 4096], F32, EI)
        ln_g = P.dram("ln_g", [1, 2048], F32, EI)
        ln_b = P.dram("ln_b", [1, 2048], F32, EI)
        wsT = P.dram("wsT", [128, 8, 128], F32, EI)
        maskT = P.dram("maskT", [128, 128], F32, EI)
        bs = P.dram("bs", [1, 1024], F32, EI)
        ya_out = P.dram("yaT", [D, 1024], BF16, EO)
        ph_gmlp(P, hn, hn_t, w_uv, ln_g, ln_b, wsT, maskT, bs, ya_out, 0, psring, consts)
    return P.finish()


def gmlp_host_inputs(inputs, j):
    ws = np.asarray(inputs["ev_gm_ws"][j])
    cid = np.arange(128) // 64
    maskT = (cid[None, :] >= cid[:, None]).astype(np.float32)
    return {
        "w_uv": np.ascontiguousarray(np.asarray(inputs["ev_w_in"][j])[:, 0:4096]),
        "ln_g": np.asarray(inputs["ev_gm_ln_g"][j]).reshape(1, 2048),
        "ln_b": np.asarray(inputs["ev_gm_ln_b"][j]).reshape(1, 2048),
        "wsT": np.ascontiguousarray(ws.transpose(2, 0, 1)),
        "maskT": maskT,
        "bs": np.asarray(inputs["ev_gm_bs"][j]).reshape(1, 1024),
    }


def vec128(v):
    v = np.asarray(v)
    return np.ascontiguousarray(v.reshape(-1, 128).T)


def build_T2(nsrc, tail):
    P = Prog()
    EI, EO = "ExternalInput", "ExternalOutput"
    W = 1026
    ys = [P.dram("y%d" % i, [D, W], BF16, EI) for i in range(nsrc)]
    hT = P.dram("hT", [D, W], F32, EI)
    flag = P.dram("flag", [128, 1], F32, EI)
    w_out = P.dram("w_out", [D * nsrc, D], F32, EI)
    g_ffn = P.dram("g_ffn", [128, NKC], F32, EI)
    w_up = P.dram("w_up", [D, 2 * DFF], F32, EI)
    cw = P.dram("cw", [128, NFC, 3], F32, EI)
    cb = P.dram("cb", [128, NFC], F32, EI)
    w_down = P.dram("w_down", [DFF, D], F32, EI)
    h_out = P.dram("h_out", [D, 1024], F32, EO)
    consts = make_consts(P)
    consts["hmid_halo"] = P.dram("hmid_halo", [D, 2], F32)
    _, psring = psum_ring(P, 8)
    flag_sb, flag_t = load_small(P, flag, [128, 1], name="flag")
    gf_sb, gf_t = load_small(P, g_ffn, [128, NKC], name="gffn")
    ph_t2(P, ys, 0, hT, 0, flag_sb, flag_t, w_out, gf_sb, gf_t, w_up, cw, cb, w_down, h_out, 0, psring, consts)
    if tail in ("norm", "norm_gmlp"):
        gain = P.dram("gain", [128, NKC], F32, EI)
        hn_out = P.dram("hnT", [D, 1024], BF16, EO)
        g_sb, g_t = load_small(P, gain, [128, NKC], name="gain")
        hn = P.sbuf("hn", [128, NKC, 1024], BF16)
        hn_t = Tok()
        ph_norm(P, h_out, [(0, 512), (512, 512)], g_sb, g_t, consts["ones"][0], consts["ones"][1], psring,
                hn_sb=hn, hn_tok=hn_t, out_dram=hn_out, out_tiles=[0, 512])
        if tail == "norm_gmlp":
            w_uv = P.dram("w_uv", [D, 4096], F32, EI)
            ln_g = P.dram("ln_g", [1, 2048], F32, EI)
            ln_b = P.dram("ln_b", [1, 2048], F32, EI)
            wsT = P.dram("wsT", [128, 8, 128], F32, EI)
            maskT = P.dram("maskT", [128, 128], F32, EI)
            bs = P.dram("bs", [1, 1024], F32, EI)
            ya_out = P.dram("yaT", [D, 1024], BF16, EO)
            ph_gmlp(P, hn, hn_t, w_uv, ln_g, ln_b, wsT, maskT, bs, ya_out, 0, psring, consts)
    elif tail == "final":
        gain = P.dram("gain", [128, NKC], F32, EI)
        fin = P.dram("finT", [D, 1024], F32, EO)
        g_sb, g_t = load_small(P, gain, [128, NKC], name="gain")
        ph_norm(P, h_out, [(0, 512), (512, 512)], g_sb, g_t, consts["ones"][0], consts["ones"][1], psring,
                out_dram=fin, out_tiles=[0, 512], out_dtype=F32)
    return P.finish()


def ffn_host_inputs(inputs, layer):
    cw = np.asarray(inputs["ff_conv_w"][layer])
    return {
        "g_ffn": vec128(inputs["norm_ffn"][layer]),
        "w_up": np.asarray(inputs["ff_w_up"][layer]),
        "cw": np.ascontiguousarray(cw.reshape(3, NFC, 128).transpose(2, 1, 0)),
        "cb": vec128(inputs["ff_conv_b"][layer]),
        "w_down": np.asarray(inputs["ff_w_down"][layer]),
    }


def bc_last(ap2, n):
    return ap2.unsqueeze(2).to_broadcast([ap2.shape[0], ap2.shape[1], n])


def bc_mid(ap2, h):
    return ap2.unsqueeze(1).to_broadcast([ap2.shape[0], h, ap2.shape[1]])


def ph_ssd(P, hn_all, w_g, cw_d, cb_d, dtb_d, alog_d, dsk_d, nw_d, tri_d, stri_d, ident_d, yb_out, out_c0, psring):
    mk = P.mark()
    NH = 8
    hsrc = hn_all.rearrange("(c p) t -> p c t", p=128)
    wsrc = w_g.rearrange("(c p) m -> p c m", p=128)
    Wfm = P.sbuf("sd_wfm", [128, NKC, 768], BF16); wfm_t = Tok()
    Wz = P.sbuf("sd_wz", [128, NKC, 512], BF16); wz_t = Tok()
    Wdt = P.sbuf("sd_wdt", [128, NKC, NH], BF16); wdt_t = Tok()
    P.pool.dma(Wfm[:], wsrc[:, :, 0:768], writes=[wfm_t])
    P.pool.dma(Wz[:], wsrc[:, :, 768:1280], writes=[wz_t])
    P.pool.dma(Wdt[:], wsrc[:, :, 1280:1288], writes=[wdt_t])
    cw, cw_t = load_small(P, cw_d, [128, 6, 4], name="sd_cw")
    cb, cb_t = load_small(P, cb_d, [128, 6], name="sd_cb")
    dtb, dtb_t = load_small(P, dtb_d.partition_broadcast(128), [128, NH], name="sd_dtb")
    a_b, a_t = load_small(P, alog_d.partition_broadcast(128), [128, NH], name="sd_a")
    dsk, dsk_t = load_small(P, dsk_d.partition_broadcast(128), [128, NH], name="sd_dsk")
    nw, nw_t = load_small(P, nw_d.partition_broadcast(128), [128, 512], name="sd_nw")
    TRI, tri_t = load_small(P, tri_d, [128, 128], name="sd_tri")
    STRI, stri_t = load_small(P, stri_d, [128, 128], name="sd_stri")
    ONESF = P.sbuf("sd_onesf", [128, 128], F32); onesf_t = Tok()
    P.dve.do(lambda e: e.memset(ONESF[:], 1.0), writes=[onesf_t])
    TRIb = P.sbuf("sd_trib", [128, 128], BF16); trib_t = Tok()
    STRIb = P.sbuf("sd_strib", [128, 128], BF16); strib_t = Tok()
    IDb = P.sbuf("sd_idb", [128, 128], BF16); idb_t = Tok()
    P.pool.dma(TRIb[:], tri_d, writes=[trib_t])
    P.pool.dma(STRIb[:], stri_d, writes=[strib_t])
    P.pool.dma(IDb[:], ident_d, writes=[idb_t])
    P.act.do(lambda e: e.activation(out=a_b[:], in_=a_b[:], func=AF.Exp), reads=[a_t], writes=[a_t])
    P.dve.do(lambda e: e.tensor_scalar(out=a_b[:], in0=a_b[:], scalar1=-1.0, scalar2=None, op0=ALU.mult), reads=[a_t], writes=[a_t])
    hnb = [(P.sbuf("sd_hn", [128, NKC, 512], BF16), Tok()) for _ in range(2)]
    RAW = P.sbuf("sd_raw", [128, 6, 515], F32); raw_t = [Tok() for _ in range(6)]
    for m in range(6):
        P.dve.do(lambda e: e.memset(RAW[:, m, 0:3], 0.0), writes=[raw_t[m]])
    ACC = [(P.sbuf("sd_acc", [128, 512], F32), Tok()) for _ in range(2)]
    XT = P.sbuf("sd_xt", [128, 6, 512], BF16); xt_t = [Tok() for _ in range(6)]
    H = P.sbuf("sd_h", [128, 512], F32); h_t = Tok()
    Hb = P.sbuf("sd_hb", [128, 512], BF16); hb_t = Tok()
    P.dve.do(lambda e: e.memset(H[:], 0.0), writes=[h_t])
    P.dve.do(lambda e: e.memset(Hb[:], 0.0), writes=[hb_t])
    YT = [(P.sbuf("sd_yt", [128, 4, 512], BF16), Tok()) for _ in range(2)]
    def dbl(name, shape, dt):
        return [(P.sbuf(name, shape, dt), Tok()) for _ in range(2)]
    xtok_b = dbl("sd_xtok", [128, 512], BF16)
    btok_b = dbl("sd_btok", [128, 128], BF16)
    zs_b = dbl("sd_zs", [128, 512], F32)
    sm_b = dbl("sd_sm", [128, 12, NH], F32)
    smb_b = dbl("sd_smb", [128, 2, NH], BF16)
    rt_b = dbl("sd_rt", [128, 2, NH, 128], BF16)
    e_b = dbl("sd_e", [128, NH, 128], BF16)
    cb_b = dbl("sd_cbt", [128, 128], BF16)
    g_b = dbl("sd_g", [128, NH, 128], BF16)
    xd_b = dbl("sd_xd", [128, 2, 512], BF16)
    t1_b = dbl("sd_t1", [128, 512], F32)
    t2_b = dbl("sd_t2", [128, 512], F32)
    yn_b = dbl("sd_yn", [128, 512], BF16)
    ss_b = dbl("sd_ss", [128, 2], F32)
    pst_xb = P.psum("sd_pstx", [128, 8, 128], BF16); pst_x_t = Tok()
    pst_x = pst_xb[:, 0:4, :]
    pst_b = pst_xb[:, 4, :]; pst_b_t = Tok()
    pst_y = P.psum("sd_psty", [128, 8, 128], BF16)[:, 0:4, :]; pst_y_t = Tok()
    ydst = yb_out.rearrange("(c p) t -> p c t", p=128)
    ci = 0
    for sc in range(4):
        hn, hn_t = hnb[sc % 2]
        for c4 in range(0, NKC, 4):
            P.sp.dma(hn[:, c4:c4 + 4, :], hsrc[:, c4:c4 + 4, sc * 512:(sc + 1) * 512], writes=[hn_t])
        for m in range(6):
            ps, pstok = psring.next()
            for k in range(NKC):
                P.pe.do(lambda e: e.matmul(ps[:, :], Wfm[:, k, m * 128:(m + 1) * 128], hn[:, k, :], start=(k == 0), stop=(k == NKC - 1)),
                        reads=[wfm_t, hn_t], writes=[pstok], inc=(k == NKC - 1))
            P.act.do(lambda e: e.activation(out=RAW[:, m, 3:515], in_=ps[:, :], func=AF.Copy), reads=[pstok], writes=[raw_t[m]])
            acc, acc_t = ACC[m % 2]
            P.dve.do(lambda e: e.tensor_scalar(out=acc[:], in0=RAW[:, m, 3:515], scalar1=cw[:, m, 3:4], scalar2=cb[:, m:m + 1],
                                               op0=ALU.mult, op1=ALU.add), reads=[raw_t[m], cw_t, cb_t], writes=[acc_t])
            for tap in range(3):
                P.dve.do(lambda e: e.scalar_tensor_tensor(out=acc[:], in0=RAW[:, m, tap:tap + 512], scalar=cw[:, m, tap:tap + 1], in1=acc[:],
                                                          op0=ALU.mult, op1=ALU.add), reads=[raw_t[m], cw_t, acc_t], writes=[acc_t])
            P.dve.do(lambda e: e.tensor_copy(out=RAW[:, m, 0:3], in_=RAW[:, m, 512:515]), reads=[raw_t[m]], writes=[raw_t[m]])
            P.act.do(lambda e: e.activation(out=XT[:, m, :], in_=acc[:], func=AF.Silu), reads=[acc_t], writes=[xt_t[m]])
        yt, yt_t = YT[sc % 2]
        for cc in range(4):
            b = ci % 2
            ci += 1
            cs = slice(cc * 128, (cc + 1) * 128)
            xtok, xtok_t = xtok_b[b]; btok, btok_t = btok_b[b]; zs, zs_t = zs_b[b]
            sm, sm_t = sm_b[b]; smb, smb_t = smb_b[b]; rt, rt_t = rt_b[b]; E, e_t = e_b[b]
            cbt, cbt_t = cb_b[b]; G, g_t = g_b[b]; xd, xd_t = xd_b[b]; t1, t1_t = t1_b[b]; t2, t2_t = t2_b[b]
            yn, yn_t = yn_b[b]; ss, ss_t = ss_b[b]
            for m in range(4):
                P.pe.do(lambda e: e.transpose(pst_x[:, m, :], XT[:, m, cs], IDb[:]), reads=[xt_t[m], idb_t], writes=[pst_x_t], inc=(m == 3))
            P.act.do(lambda e: e.activation(out=xtok[:], in_=pst_xb[:, 0:4, :].rearrange("p a b -> p (a b)"), func=AF.Copy),
                     reads=[pst_x_t], writes=[xtok_t])
            P.pe.do(lambda e: e.transpose(pst_b, XT[:, 4, cs], IDb[:]), reads=[xt_t[4], idb_t], writes=[pst_b_t])
            P.act.do(lambda e: e.activation(out=btok[:], in_=pst_b, func=AF.Copy), reads=[pst_b_t], writes=[btok_t])
            psz, psz_t = psring.next()
            for k in range(NKC):
                P.pe.do(lambda e: e.matmul(psz[:, :], hn[:, k, cs], Wz[:, k, :], start=(k == 0), stop=(k == NKC - 1)),
                        reads=[wz_t, hn_t], writes=[psz_t], inc=(k == NKC - 1))
            P.act.do(lambda e: e.activation(out=zs[:], in_=psz[:, :], func=AF.Silu), reads=[psz_t], writes=[zs_t])
            psd, psd_t = psring.next()
            for k in range(NKC):
                P.pe.do(lambda e: e.matmul(psd[:, 0:NH], hn[:, k, cs], Wdt[:, k, :], start=(k == 0), stop=(k == NKC - 1)),
                        reads=[wdt_t, hn_t], writes=[psd_t], inc=(k == NKC - 1))
            P.dve.do(lambda e: e.tensor_tensor(out=sm[:, 0, :], in0=psd[:, 0:NH], in1=dtb[:], op=ALU.add), reads=[psd_t, dtb_t], writes=[sm_t])
            P.act.do(lambda e: e.activation(out=sm[:, 1, :], in_=sm[:, 0, :], func=AF.Abs), reads=[sm_t], writes=[sm_t])
            P.act.do(lambda e: e.activation(out=sm[:, 2, :], in_=sm[:, 1, :], func=AF.Exp, scale=-1.0), reads=[sm_t], writes=[sm_t])
            P.act.do(lambda e: e.activation(out=sm[:, 3, :], in_=sm[:, 2, :], func=AF.Ln, bias=1.0), reads=[sm_t], writes=[sm_t])
            P.dve.do(lambda e: e.scalar_tensor_tensor(out=sm[:, 4, :], in0=sm[:, 0, :], scalar=0.0, in1=sm[:, 3, :], op0=ALU.max, op1=ALU.add),
                     reads=[sm_t], writes=[sm_t])
            P.dve.do(lambda e: e.tensor_tensor(out=sm[:, 5, :], in0=sm[:, 4, :], in1=a_b[:], op=ALU.mult), reads=[sm_t, a_t], writes=[sm_t])
            P.dve.do(lambda e: e.tensor_copy(out=smb[:, 0, :], in_=sm[:, 5, :]), reads=[sm_t], writes=[smb_t])
            P.dve.do(lambda e: e.tensor_tensor(out=sm[:, 6, :], in0=sm[:, 5, :], in1=smb[:, 0, :], op=ALU.subtract), reads=[sm_t, smb_t], writes=[sm_t])
            P.dve.do(lambda e: e.tensor_copy(out=smb[:, 1, :], in_=sm[:, 6, :]), reads=[sm_t], writes=[smb_t])
            pss, pss_t = psring.next()
            P.pe.do(lambda e: e.matmul(pss[:, 0:NH], TRI[:], sm[:, 5, :], start=True, stop=True), reads=[tri_t, sm_t], writes=[pss_t], inc=False)
            P.pe.do(lambda e: e.matmul(pss[:, NH:2 * NH], STRI[:], sm[:, 5, :], start=True, stop=True), reads=[stri_t, sm_t], writes=[pss_t], inc=False)
            P.pe.do(lambda e: e.matmul(pss[:, 2 * NH:3 * NH], ONESF[:], sm[:, 5, :], start=True, stop=True), reads=[onesf_t, sm_t], writes=[pss_t])
            P.act.do(lambda e: e.activation(out=sm[:, 7:10, :].rearrange("p a b -> p (a b)"), in_=pss[:, 0:3 * NH], func=AF.Exp),
                     reads=[pss_t], writes=[sm_t])
            P.dve.do(lambda e: e.tensor_tensor(out=sm[:, 10, :], in0=sm[:, 4, :], in1=sm[:, 8, :], op=ALU.mult), reads=[sm_t], writes=[sm_t])
            for hl in range(2):
                P.dve.do(lambda e: e.tensor_tensor(out=rt[:, hl, :, :], in0=bc_mid(TRIb[:], NH), in1=bc_last(smb[:, hl, :], 128), op=ALU.mult),
                         reads=[trib_t, smb_t], writes=[rt_t])
            dps = []
            for hf in range(2):
                ps, pstok = psring.next()
                for hl in range(2):
                    P.pe.do(lambda e: e.matmul(ps[:, :], STRIb[:], rt[:, hl, hf * 4:(hf + 1) * 4, :].rearrange("p a b -> p (a b)"),
                                               start=(hl == 0), stop=(hl == 1)), reads=[strib_t, rt_t], writes=[pstok], inc=(hl == 1))
                dps.append((ps, pstok))
            for hf in range(2):
                ps, pstok = dps[hf]
                P.act.do(lambda e: e.activation(out=E[:, hf * 4:(hf + 1) * 4, :].rearrange("p a b -> p (a b)"), in_=ps[:, :], func=AF.Exp),
                         reads=[pstok], writes=[e_t])
            psc, psc_t = psring.next()
            P.pe.do(lambda e: e.matmul(psc[:, 0:128], XT[:, 4, cs], XT[:, 5, cs], start=True, stop=True), reads=[xt_t[4], xt_t[5]], writes=[psc_t])
            P.dve.do(lambda e: e.tensor_tensor(out=cbt[:], in0=psc[:, 0:128], in1=TRI[:], op=ALU.mult), reads=[psc_t, tri_t], writes=[cbt_t])
            P.dve.do(lambda e: e.tensor_tensor(out=G[:], in0=E[:], in1=bc_mid(cbt[:], NH), op=ALU.mult), reads=[e_t, cbt_t], writes=[g_t])
            x3 = xtok[:].rearrange("p (h d) -> p h d", h=NH)
            P.dve.do(lambda e: e.tensor_tensor(out=xd[:, 0, :].rearrange("p (h d) -> p h d", h=NH), in0=x3, in1=bc_last(sm[:, 4, :], 64), op=ALU.mult),
                     reads=[xtok_t, sm_t], writes=[xd_t])
            P.dve.do(lambda e: e.tensor_tensor(out=xd[:, 1, :].rearrange("p (h d) -> p h d", h=NH), in0=x3, in1=bc_last(sm[:, 10, :], 64), op=ALU.mult),
                     reads=[xtok_t, sm_t], writes=[xd_t])
            psy, psy_t = psring.next()
            for h in range(NH):
                P.pe.do(lambda e: e.matmul(psy[:, h * 64:(h + 1) * 64], G[:, h, :], xd[:, 0, h * 64:(h + 1) * 64], start=True, stop=True),
                        reads=[g_t, xd_t], writes=[psy_t], inc=(h == NH - 1))
            pso, pso_t = psring.next()
            P.pe.do(lambda e: e.matmul(pso[:, :], XT[:, 5, cs], Hb[:], start=True, stop=True), reads=[xt_t[5], hb_t], writes=[pso_t])
            psst, psst_t = psring.next()
            P.pe.do(lambda e: e.matmul(psst[:, :], btok[:], xd[:, 1, :], start=True, stop=True), reads=[btok_t, xd_t], writes=[psst_t])
            P.dve.do(lambda e: e.tensor_tensor(out=t1[:].rearrange("p (h d) -> p h d", h=NH), in0=pso[:, :].rearrange("p (h d) -> p h d", h=NH),
                                               in1=bc_last(sm[:, 7, :], 64), op=ALU.mult), reads=[pso_t, sm_t], writes=[t1_t])
            P.dve.do(lambda e: e.tensor_tensor(out=t1[:], in0=t1[:], in1=psy[:, :], op=ALU.add), reads=[t1_t, psy_t], writes=[t1_t])
            P.dve.do(lambda e: e.tensor_tensor(out=t2[:].rearrange("p (h d) -> p h d", h=NH), in0=x3, in1=bc_last(dsk[:], 64), op=ALU.mult),
                     reads=[xtok_t, dsk_t], writes=[t2_t])
            P.dve.do(lambda e: e.tensor_tensor(out=t1[:], in0=t1[:], in1=t2[:], op=ALU.add), reads=[t1_t, t2_t], writes=[t1_t])
            P.dve.do(lambda e: e.tensor_tensor(out=H[:].rearrange("p (h d) -> p h d", h=NH), in0=H[:].rearrange("p (h d) -> p h d", h=NH),
                                               in1=bc_last(sm[:, 9, :], 64), op=ALU.mult), reads=[h_t, sm_t], writes=[h_t])
            P.dve.do(lambda e: e.tensor_tensor(out=H[:], in0=H[:], in1=psst[:, :], op=ALU.add), reads=[h_t, psst_t], writes=[h_t])
            P.act.do(lambda e: e.activation(out=Hb[:], in_=H[:], func=AF.Copy), reads=[h_t], writes=[hb_t])
            P.dve.do(lambda e: e.tensor_tensor(out=t1[:], in0=t1[:], in1=zs[:], op=ALU.mult), reads=[t1_t, zs_t], writes=[t1_t])
            P.act.do(lambda e: e.activation(out=t2[:], in_=t1[:], func=AF.Square, accum_out=ss[:, 0:1]), reads=[t1_t], writes=[t2_t, ss_t])
            P.act.do(lambda e: e.activation(out=ss[:, 1:2], in_=ss[:, 0:1], func=AF.Sqrt, scale=1.0 / 512, bias=EPS), reads=[ss_t], writes=[ss_t])
            P.dve.do(lambda e: e.reciprocal(out=ss[:, 1:2], in_=ss[:, 1:2]), reads=[ss_t], writes=[ss_t])
            P.dve.do(lambda e: e.scalar_tensor_tensor(out=yn[:], in0=t1[:], scalar=ss[:, 1:2], in1=nw[:], op0=ALU.mult, op1=ALU.mult),
                     reads=[t1_t, ss_t, nw_t], writes=[yn_t])
            for m in range(4):
                P.pe.do(lambda e: e.transpose(pst_y[:, m, :], yn[:, m * 128:(m + 1) * 128], IDb[:]), reads=[yn_t, idb_t], writes=[pst_y_t], inc=(m == 3))
            P.act.do(lambda e: e.activation(out=yt[:, :, cs], in_=pst_y, func=AF.Copy), reads=[pst_y_t], writes=[yt_t])
        P.sp.dma(ydst[:, :, out_c0 + sc * 512:out_c0 + (sc + 1) * 512], yt[:], reads=[yt_t])
    P.release(mk)


def tri_consts():
    k = np.arange(128)
    tri = (k[:, None] <= k[None, :]).astype(np.float32)
    stri = (k[:, None] > k[None, :]).astype(np.float32)
    return tri, stri, np.eye(128, dtype=np.float32)


def build_SSD(ngroups=1):
    P = Prog()
    EI, EO = "ExternalInput", "ExternalOutput"
    hn_all = P.dram("hn_all", [D, SEQ], BF16, EI)
    tri = P.dram("tri", [128, 128], F32, EI)
    stri = P.dram("stri", [128, 128], F32, EI)
    ident = P.dram("ident", [128, 128], F32, EI)
    yb = P.dram("ybT", [512 * ngroups, SEQ], BF16, EO)
    _, psring = psum_ring(P, 6)
    for g in range(ngroups):
        w_g = P.dram("w_g%d" % g, [D, 1288], F32, EI)
        cw = P.dram("scw%d" % g, [128, 6, 4], F32, EI)
        cb = P.dram("scb%d" % g, [128, 6], F32, EI)
        dtb = P.dram("dtb%d" % g, [1, 8], F32, EI)
        alog = P.dram("alog%d" % g, [1, 8], F32, EI)
        dsk = P.dram("dsk%d" % g, [1, 8], F32, EI)
        nw = P.dram("nw%d" % g, [1, 512], F32, EI)
        ph_ssd(P, hn_all, w_g, cw, cb, dtb, alog, dsk, nw, tri, stri, ident, yb[g * 512:(g + 1) * 512, :], 0, psring)
    return P.finish()


def ssd_host_inputs(inputs, j, g, sfx=""):
    w_in = np.asarray(inputs["ev_w_in"][j])
    o2 = 4096
    o3 = 6144
    o4 = 9216
    xs = slice(o3 + g * 512, o3 + (g + 1) * 512)
    bsl = slice(o3 + 2048 + g * 128, o3 + 2048 + (g + 1) * 128)
    csl = slice(o3 + 2560 + g * 128, o3 + 2560 + (g + 1) * 128)
    zsl = slice(o2 + g * 512, o2 + (g + 1) * 512)
    dsl = slice(o4 + g * 8, o4 + (g + 1) * 8)
    w_g = np.concatenate([w_in[:, xs], w_in[:, bsl], w_in[:, csl], w_in[:, zsl], w_in[:, dsl]], axis=1)
    cwf = np.asarray(inputs["ev_conv_w"][j])
    cbf = np.asarray(inputs["ev_conv_b"][j])
    ch = np.concatenate([np.arange(g * 512, (g + 1) * 512), 2048 + np.arange(g * 128, (g + 1) * 128),
                         2560 + np.arange(g * 128, (g + 1) * 128)])
    cw = cwf[:, ch]
    cw = np.ascontiguousarray(cw.reshape(4, 6, 128).transpose(2, 1, 0))
    cb = np.ascontiguousarray(cbf[ch].reshape(6, 128).T)
    hs = slice(g * 8, (g + 1) * 8)
    return {
        "w_g" + sfx: np.ascontiguousarray(w_g),
        "scw" + sfx: cw, "scb" + sfx: cb,
        "dtb" + sfx: np.asarray(inputs["ev_dt_bias"][j])[hs].reshape(1, 8),
        "alog" + sfx: np.asarray(inputs["ev_a_log"][j])[hs].reshape(1, 8),
        "dsk" + sfx: np.asarray(inputs["ev_d_skip"][j])[hs].reshape(1, 8),
        "nw" + sfx: np.asarray(inputs["ev_ssm_norm_w"][j])[g * 512:(g + 1) * 512].reshape(1, 512),
    }


TWO_PI = 6.283185307179586
CW1 = 6.28125
CW2 = TWO_PI - CW1
QK_SCALE = 192.0 ** -0.5


def ph_mla(P, hn_all, w_in, qn_d, kvn_d, wq_list, wk_list, wv_list, pos_d, invf_d, sgn_d, swap_d, neg_d, ident_d, o_out, consts, stage=9):
    ones_bf, ones_tok = consts["ones"]
    mk0 = P.mark()
    psall = P.psum("mla_ps", [128, 8, 512], F32)
    psA = Ring([(psall[:, b, :], Tok()) for b in range(4)])
    psO = Ring([(psall[:, 4 + b, :], Tok()) for b in range(4)])
    hsrc = hn_all.rearrange("(c p) t -> p c t", p=128)
    COS2 = P.sbuf("ml_cos", [128, SEQ], F32); cos_t = Tok()
    SIN2 = P.sbuf("ml_sin", [128, SEQ], F32); sin_t = Tok()
    cqg = P.sbuf("ml_cqg", [128, 4, SEQ], BF16); cqg_t = Tok()
    ckvg = P.sbuf("ml_ckvg", [128, 4, SEQ], BF16); ckvg_t = Tok()
    kpe2 = P.sbuf("ml_kpe", [128, SEQ], BF16); kpe_t = Tok()
    kpeA = P.sbuf("ml_kpeA", [128, SEQ], BF16); kpeA_t = Tok()
    kpeB = P.sbuf("ml_kpeB", [128, SEQ], BF16); kpeB_t = Tok()
    RQ = P.sbuf("ml_rq", [128, SEQ], F32); rq_t = Tok()
    RKV = P.sbuf("ml_rkv", [128, SEQ], F32); rkv_t = Tok()
    SWb = P.sbuf("ml_swb", [128, 128], BF16); swb_t = Tok()
    NEGb = P.sbuf("ml_negb", [128, 128], BF16); negb_t = Tok()
    IDb = P.sbuf("ml_idb", [128, 128], BF16); idb_t = Tok()
    P.pool.dma(SWb[:], swap_d, writes=[swb_t])
    P.pool.dma(NEGb[:], neg_d, writes=[negb_t])
    P.pool.dma(IDb[:], ident_d, writes=[idb_t])
    mk = P.mark()
    posi = P.sbuf("ml_posi", [128, SEQ], I32); posi_t = Tok()
    ang = P.sbuf("ml_ang", [128, SEQ], F32); ang_t = Tok()
    kf = P.sbuf("ml_kf", [128, SEQ], F32); kf_t = Tok()
    ki = P.sbuf("ml_ki", [128, SEQ], I32); ki_t = Tok()
    rr = P.sbuf("ml_rr", [128, SEQ], F32); rr_t = Tok()
    invf, invf_t = load_small(P, invf_d, [128, 1], name="ml_invf")
    sgn, sgn_t = load_small(P, sgn_d, [128, 1], name="ml_sgn")
    P.sp.dma(posi[:], pos_d.partition_broadcast(128), writes=[posi_t])
    P.dve.do(lambda e: e.tensor_copy(out=ang[:], in_=posi[:]), reads=[posi_t], writes=[ang_t])
    P.dve.do(lambda e: e.tensor_scalar(out=ang[:], in0=ang[:], scalar1=invf[:, 0:1], scalar2=None, op0=ALU.mult), reads=[ang_t, invf_t], writes=[ang_t])
    for which in range(2 if stage != 0.71 else 0):
        if which == 1:
            P.dve.do(lambda e: e.tensor_scalar(out=ang[:], in0=ang[:], scalar1=float(np.pi / 2), scalar2=None, op0=ALU.add), reads=[ang_t], writes=[ang_t])
        P.dve.do(lambda e: e.tensor_scalar(out=kf[:], in0=ang[:], scalar1=float(1.0 / TWO_PI), scalar2=None, op0=ALU.mult), reads=[ang_t], writes=[kf_t])
        P.dve.do(lambda e: e.tensor_copy(out=ki[:], in_=kf[:]), reads=[kf_t], writes=[ki_t])
        P.dve.do(lambda e: e.tensor_copy(out=kf[:], in_=ki[:]), reads=[ki_t], writes=[kf_t])
        P.dve.do(lambda e: e.scalar_tensor_tensor(out=rr[:], in0=kf[:], scalar=-CW1, in1=ang[:], op0=ALU.mult, op1=ALU.add), reads=[kf_t, ang_t], writes=[rr_t])
        P.dve.do(lambda e: e.scalar_tensor_tensor(out=rr[:], in0=kf[:], scalar=-CW2, in1=rr[:], op0=ALU.mult, op1=ALU.add), reads=[kf_t, rr_t], writes=[rr_t])
        P.dve.do(lambda e: e.tensor_scalar(out=rr[:], in0=rr[:], scalar1=float(-np.pi), scalar2=float(np.pi), op0=ALU.max, op1=ALU.min), reads=[rr_t], writes=[rr_t])
        if which == 0:
            P.act.do(lambda e: e.activation(out=SIN2[:], in_=rr[:], func=AF.Sin, scale=sgn[:, 0:1]), reads=[rr_t, sgn_t], writes=[sin_t])
        else:
            P.act.do(lambda e: e.activation(out=COS2[:], in_=rr[:], func=AF.Sin), reads=[rr_t], writes=[cos_t])
    P.release(mk)
    if stage < 0.4:
        P.release(mk0)
        return
    mk = P.mark()
    wsrc = w_in.rearrange("(c p) m -> p c m", p=128)
    Win = P.sbuf("ml_win", [128, NKC, 1152], BF16); win_t = Tok()
    P.pool.dma(Win[:], wsrc, writes=[win_t])
    qn, qn_t = load_small(P, qn_d, [128, 4], name="ml_qn")
    kvn, kvn_t = load_small(P, kvn_d, [128, 4], name="ml_kvn")
    hnb = [(P.sbuf("ml_hn", [128, NKC, 512], BF16), Tok()) for _ in range(2)]
    sqb = [(P.sbuf("ml_sq", [128, 512], BF16), Tok()) for _ in range(3)]
    krb = [(P.sbuf("ml_krb", [128, 512], BF16), Tok()) for _ in range(2)]
    tA = [(P.sbuf("ml_tA", [128, 512], F32), Tok()) for _ in range(2)]
    tB = [(P.sbuf("ml_tB", [128, 512], F32), Tok()) for _ in range(2)]
    sqi = 0
    for tt in range(4):
        ts = slice(tt * 512, (tt + 1) * 512)
        hn, hn_t = hnb[tt % 2]
        for c4 in range(0, NKC, 4):
            P.sp.dma(hn[:, c4:c4 + 4, :], hsrc[:, c4:c4 + 4, ts], writes=[hn_t])
        if stage == 0.5:
            continue
        for part in range(2 if stage not in (0.7, 0.71) else 1):
            dst, dst_t, gn, gn_t, R, R_t = (cqg, cqg_t, qn, qn_t, RQ, rq_t) if part == 0 else (ckvg, ckvg_t, kvn, kvn_t, RKV, rkv_t)
            pss, pss_t = psO.next()
            for m in range(4):
                ps, pstok = psA.next()
                for k in range(NKC):
                    P.pe.do(lambda e: e.matmul(ps[:, :], Win[:, k, (part * 4 + m) * 128:(part * 4 + m + 1) * 128], hn[:, k, :],
                                               start=(k == 0), stop=(k == NKC - 1)), reads=[win_t, hn_t], writes=[pstok], inc=(k == NKC - 1))
                s, s_t = sqb[sqi % 3]
                sqi += 1
                P.act.do(lambda e: e.activation(out=s[:], in_=ps[:, :], func=AF.Square), reads=[pstok], writes=[s_t])
                P.dve.do(lambda e: e.tensor_scalar(out=dst[:, m, ts], in0=ps[:, :], scalar1=gn[:, m:m + 1], scalar2=None, op0=ALU.mult),
                         reads=[pstok, gn_t, s_t], writes=[dst_t])
                P.pe.do(lambda e: e.matmul(pss[:, :], ones_bf[:, :], s[:], start=(m == 0), stop=(m == 3)), reads=[s_t, ones_tok], writes=[pss_t], inc=True)
            P.act.do(lambda e: e.activation(out=R[:, ts], in_=pss[:, :], func=AF.Sqrt, scale=1.0 / 512, bias=EPS), reads=[pss_t], writes=[R_t])
            P.dve.do(lambda e: e.reciprocal(out=R[:, ts], in_=R[:, ts]), reads=[R_t], writes=[R_t])
            for m in range(4):
                P.dve.do(lambda e: e.tensor_tensor(out=dst[:, m, ts], in0=dst[:, m, ts], in1=R[:, ts], op=ALU.mult), reads=[dst_t, R_t], writes=[dst_t])
        if stage in (1, 0.5, 0.7, 0.71):
            continue
        ps, pstok = psA.next()
        for k in range(NKC):
            P.pe.do(lambda e: e.matmul(ps[:, :], Win[:, k, 1024:1152], hn[:, k, :], start=(k == 0), stop=(k == NKC - 1)),
                    reads=[win_t, hn_t], writes=[pstok], inc=(k == NKC - 1))
        kb_, kb_t = krb[tt % 2]
        a_, a_t = tA[tt % 2]
        b_, b_t = tB[tt % 2]
        P.act.do(lambda e: e.activation(out=kb_[:], in_=ps[:, :], func=AF.Copy), reads=[pstok], writes=[kb_t])
        ps2, ps2_t = psA.next()
        P.pe.do(lambda e: e.matmul(ps2[:, :], SWb[:], kb_[:], start=True, stop=True), reads=[swb_t, kb_t], writes=[ps2_t])
        P.dve.do(lambda e: e.tensor_tensor(out=a_[:], in0=ps[:, :], in1=COS2[:, ts], op=ALU.mult), reads=[pstok, cos_t, kb_t], writes=[a_t])
        P.dve.do(lambda e: e.tensor_tensor(out=b_[:], in0=ps2[:, :], in1=SIN2[:, ts], op=ALU.mult), reads=[ps2_t, sin_t], writes=[b_t])
        P.dve.do(lambda e: e.tensor_tensor(out=kpe2[:, ts], in0=a_[:], in1=b_[:], op=ALU.add), reads=[a_t, b_t], writes=[kpe_t])
    P.dve.do(lambda e: e.memset(kpeA[:], 0.0), writes=[kpeA_t])
    P.dve.do(lambda e: e.memset(kpeB[:], 0.0), writes=[kpeB_t])
    P.dve.do(lambda e: e.tensor_copy(out=kpeA[0:64, :], in_=kpe2[0:64, :]), reads=[kpe_t, kpeA_t], writes=[kpeA_t])
    P.dve.do(lambda e: e.tensor_copy(out=kpeB[64:128, :], in_=kpe2[64:128, :]), reads=[kpe_t, kpeB_t], writes=[kpeB_t])
    P.release(mk)
    if stage < 2:
        P.release(mk0)
        return
    odst = o_out.rearrange("(c p) t -> p c t", p=128)
    for hg in range(len(wq_list)):
        mk = P.mark()
        Wq = P.sbuf("ml_wq", [128, 4, 768], BF16); wq_t = Tok()
        Wk = P.sbuf("ml_wk", [128, 4, 512], BF16); wk_t = Tok()
        Wv = P.sbuf("ml_wv", [128, 4, 512], BF16); wv_t = Tok()
        P.pool.dma(Wq[:], wq_list[hg].rearrange("(c p) m -> p c m", p=128), writes=[wq_t])
        P.pool.dma(Wk[:], wk_list[hg].rearrange("(c p) m -> p c m", p=128), writes=[wk_t])
        P.pool.dma(Wv[:], wv_list[hg].rearrange("(c p) m -> p c m", p=128), writes=[wv_t])
        QN = P.sbuf("ml_qnope", [128, 4, SEQ], BF16); qnope_t = Tok()
        QP = P.sbuf("ml_qpe", [128, 2, SEQ], BF16); qpe_t = Tok()
        KN = P.sbuf("ml_knope", [128, 4, SEQ], BF16); knope_t = Tok()
        V = P.sbuf("ml_v", [128, 16, 512], BF16); v_t = Tok()
        qb = [(P.sbuf("ml_qb", [128, 512], BF16), Tok()) for _ in range(2)]
        tA = [(P.sbuf("ml_tA2", [128, 512], F32), Tok()) for _ in range(2)]
        tB = [(P.sbuf("ml_tB2", [128, 512], F32), Tok()) for _ in range(2)]
        pT = [(P.sbuf("ml_pT", [128, 512], BF16), Tok()) for _ in range(3)]
        rec = [(P.sbuf("ml_rec", [128, 512], F32), Tok()) for _ in range(2)]
        ot = [(P.sbuf("ml_ot", [128, 512], BF16), Tok()) for _ in range(2)]
        for tt in range(4):
            ts = slice(tt * 512, (tt + 1) * 512)
            for m in range(6):
                ps, pstok = psA.next()
                for k in range(4):
                    P.pe.do(lambda e: e.matmul(ps[:, :], Wq[:, k, m * 128:(m + 1) * 128], cqg[:, k, ts], start=(k == 0), stop=(k == 3)),
                            reads=[wq_t, cqg_t], writes=[pstok], inc=(k == 3))
                if m < 4:
                    P.act.do(lambda e: e.activation(out=QN[:, m, ts], in_=ps[:, :], func=AF.Copy, scale=QK_SCALE), reads=[pstok], writes=[qnope_t])
                else:
                    q_, q_t = qb[m % 2]
                    a_, a_t = tA[m % 2]
                    b_, b_t = tB[m % 2]
                    P.act.do(lambda e: e.activation(out=q_[:], in_=ps[:, :], func=AF.Copy, scale=QK_SCALE), reads=[pstok], writes=[q_t])
                    ps2, ps2_t = psA.next()
                    P.pe.do(lambda e: e.matmul(ps2[:, :], SWb[:], q_[:], start=True, stop=True), reads=[swb_t, q_t], writes=[ps2_t])
                    P.dve.do(lambda e: e.tensor_tensor(out=a_[:], in0=q_[:], in1=COS2[:, ts], op=ALU.mult), reads=[q_t, cos_t], writes=[a_t])
                    P.dve.do(lambda e: e.tensor_tensor(out=b_[:], in0=ps2[:, :], in1=SIN2[:, ts], op=ALU.mult), reads=[ps2_t, sin_t], writes=[b_t])
                    P.dve.do(lambda e: e.tensor_tensor(out=QP[:, m - 4, ts], in0=a_[:], in1=b_[:], op=ALU.add), reads=[a_t, b_t], writes=[qpe_t])
            for m in range(4):
                ps, pstok = psA.next()
                for k in range(4):
                    P.pe.do(lambda e: e.matmul(ps[:, :], Wk[:, k, m * 128:(m + 1) * 128], ckvg[:, k, ts], start=(k == 0), stop=(k == 3)),
                            reads=[wk_t, ckvg_t], writes=[pstok], inc=(k == 3))
                P.act.do(lambda e: e.activation(out=KN[:, m, ts], in_=ps[:, :], func=AF.Copy), reads=[pstok], writes=[knope_t])
        for tb in range(16):
            ps, pstok = psA.next()
            for k in range(4):
                P.pe.do(lambda e: e.matmul(ps[:, :], ckvg[:, k, tb * 128:(tb + 1) * 128], Wv[:, k, :], start=(k == 0), stop=(k == 3)),
                        reads=[wv_t, ckvg_t], writes=[pstok], inc=(k == 3))
            P.dve.do(lambda e: e.tensor_copy(out=V[:, tb, :], in_=ps[:, :]), reads=[pstok], writes=[v_t])
        if stage < 3:
            P.release(mk)
            continue
        it = 0
        for h in range(4):
            pr, ph = h // 2, (h % 2) * 64
            for T in range(4):
                pso, pso_t = psO.next()
                psd, psd_t = psO.next()
                nkb = 4 * T + 4
                for kb in range(nkb):
                    q0 = T * 512 if kb < 4 * T else kb * 128
                    n = (T + 1) * 512 - q0
                    off = q0 - T * 512
                    ks = slice(kb * 128, (kb + 1) * 128)
                    diag = kb >= 4 * T
                    ps, pstok = psA.next()
                    P.pe.do(lambda e: e.matmul(ps[:, 0:n], KN[:, h, ks], QN[:, h, q0:q0 + n], start=True, stop=False),
                            reads=[knope_t, qnope_t], writes=[pstok], inc=False)
                    kz, kz_t = (kpeA, kpeA_t) if ph == 0 else (kpeB, kpeB_t)
                    P.pe.do(lambda e: e.matmul(ps[:, 0:n], kz[:, ks], QP[:, pr, q0:q0 + n], start=False, stop=not diag),
                            reads=[kz_t, qpe_t], writes=[pstok], inc=not diag)
                    if diag:
                        P.pe.do(lambda e: e.matmul(ps[:, 0:128], IDb[:], NEGb[:], start=False, stop=True),
                                reads=[idb_t, negb_t], writes=[pstok], inc=True)
                    p_, p_t = pT[it % 3]
                    it += 1
                    P.act.do(lambda e: e.activation(out=p_[:, 0:n], in_=ps[:, 0:n], func=AF.Exp), reads=[pstok], writes=[p_t])
                    P.pe.do(lambda e: e.matmul(pso[:, off:off + n], V[:, kb, h * 128:(h + 1) * 128], p_[:, 0:n], start=(kb == 0), stop=(kb == nkb - 1)),
                            reads=[v_t, p_t], writes=[pso_t], inc=False)
                    P.pe.do(lambda e: e.matmul(psd[:, off:off + n], ones_bf[:, :], p_[:, 0:n], start=(kb == 0), stop=(kb == nkb - 1)),
                            reads=[ones_tok, p_t], writes=[psd_t], inc=True)
                r_, r_t = rec[T % 2]
                o_, o_t = ot[T % 2]
                P.dve.do(lambda e: e.reciprocal(out=r_[:], in_=psd[:, :]), reads=[psd_t], writes=[r_t])
                P.dve.do(lambda e: e.tensor_tensor(out=o_[:], in0=pso[:, :], in1=r_[:], op=ALU.mult), reads=[pso_t, r_t], writes=[o_t])
                P.sp.dma(odst[:, hg * 4 + h, T * 512:(T + 1) * 512], o_[:], reads=[o_t])
        P.release(mk)
    P.release(mk0)


def mla_consts():
    p = np.arange(128)
    invf = (10000.0 ** (-np.arange(0, 64, 2, dtype=np.float32) / 64)).astype(np.float32)
    invf128 = invf[p % 32].reshape(128, 1).astype(np.float32)
    sgn = np.where((p % 64) < 32, -1.0, 1.0).astype(np.float32).reshape(128, 1)
    sw = np.zeros((128, 128), np.float32)
    for r in range(128):
        blk = (r // 64) * 64
        rr = r % 64
        sw[blk + (rr + 32) % 64, r] = 1.0
    neg = np.zeros((128, 128), np.float32)
    neg[64:128, 0:64] = -30000.0
    return {"invf": invf128, "sgn": sgn, "swap": sw, "neg": neg, "ident": np.eye(128, dtype=np.float32)}


def build_MLA(ngroups, stage=9):
    P = Prog()
    EI, EO = "ExternalInput", "ExternalOutput"
    hn_all = P.dram("hn_all", [D, SEQ], BF16, EI)
    w_in = P.dram("w_in", [D, 1152], F32, EI)
    qn = P.dram("qn", [128, 4], F32, EI)
    kvn = P.dram("kvn", [128, 4], F32, EI)
    pos = P.dram("pos", [1, SEQ], I32, EI)
    invf = P.dram("invf", [128, 1], F32, EI)
    sgn = P.dram("sgn", [128, 1], F32, EI)
    swap = P.dram("swap", [128, 128], F32, EI)
    neg = P.dram("neg", [128, 128], F32, EI)
    ident = P.dram("ident", [128, 128], F32, EI)
    wq = [P.dram("wq%d" % g, [512, 768], F32, EI) for g in range(ngroups)]
    wk = [P.dram("wk%d" % g, [512, 512], F32, EI) for g in range(ngroups)]
    wv = [P.dram("wv%d" % g, [512, 512], F32, EI) for g in range(ngroups)]
    o_out = P.dram("oT", [ngroups * 512, SEQ], BF16, EO)
    consts = make_consts(P)
    ph_mla(P, hn_all, w_in, qn, kvn, wq, wk, wv, pos, invf, sgn, swap, neg, ident, o_out, consts, stage=stage)
    return P.finish()


def mla_host_inputs(inputs, j, heads):
    w_in = np.asarray(inputs["od_w_in"][j])
    w_uq = np.asarray(inputs["od_w_uq"][j])
    w_ukv = np.asarray(inputs["od_w_ukv"][j])
    out = {
        "w_in": np.ascontiguousarray(np.concatenate([w_in, w_in[:, 1024:1088]], axis=1)),
        "qn": vec128(inputs["od_q_norm"][j]),
        "kvn": vec128(inputs["od_kv_norm"][j]),
    }
    for g in range(len(heads) // 4):
        hs = heads[g * 4:(g + 1) * 4]
        nope = [w_uq[:, h * 192:h * 192 + 128] for h in hs]
        pe = [w_uq[:, h * 192 + 128:h * 192 + 192] for h in hs]
        out["wq%d" % g] = np.ascontiguousarray(np.concatenate(nope + pe, axis=1))
        out["wk%d" % g] = np.ascontiguousarray(np.concatenate([w_ukv[:, h * 256:h * 256 + 128] for h in hs], axis=1))
        out["wv%d" % g] = np.ascontiguousarray(np.concatenate([w_ukv[:, h * 256 + 128:h * 256 + 256] for h in hs], axis=1))
    return out


_PROGS = {}


def _prog(key, fn):
    if key not in _PROGS:
        _PROGS[key] = fn()
    return _PROGS[key]


def _run(nc, maps):
    res = run_bass_kernel_spmd(nc, maps, core_ids=list(range(8)))
    return res.results


def _halo_cols(full, half, dtype):
    if half == 0:
        return np.ascontiguousarray(np.concatenate([np.zeros((full.shape[0], 2), dtype), full[:, 0:1024]], axis=1))
    return np.ascontiguousarray(full[:, 1022:2048])


def kernel(**inputs):
    inputs = {k: np.asarray(v) for k, v in inputs.items()}
    x = inputs["x"]
    B = x.shape[0]
    assert B == 4
    tri, stri, ident = tri_consts()
    mcon = mla_consts()
    flag = [np.full((128, 1), float(h), np.float32) for h in range(2)]
    h_full = [np.ascontiguousarray(x[b].T) for b in range(B)]

    gi = gmlp_host_inputs(inputs, 0)
    maps = []
    for c in range(8):
        b, half = c // 2, c % 2
        m = {"hT": np.ascontiguousarray(h_full[b][:, half * 1024:(half + 1) * 1024]), "gain": vec128(inputs["norm_mix"][0])}
        m.update(gi)
        maps.append(m)
    r = _run(_prog("A", lambda: build_A(True)), maps)
    hn_full = [np.concatenate([np.asarray(r[2 * b]["hnT"]), np.asarray(r[2 * b + 1]["hnT"])], axis=1) for b in range(B)]
    ya_full = [np.concatenate([np.asarray(r[2 * b]["yaT"]), np.asarray(r[2 * b + 1]["yaT"])], axis=1) for b in range(B)]

    out = None
    for layer in range(4):
        j = layer // 2
        if layer % 2 == 0:
            maps = []
            for c in range(8):
                b, half = c // 2, c % 2
                m = {"hn_all": np.ascontiguousarray(hn_full[b]), "tri": tri, "stri": stri, "ident": ident}
                m.update(ssd_host_inputs(inputs, j, 2 * half, "0"))
                m.update(ssd_host_inputs(inputs, j, 2 * half + 1, "1"))
                maps.append(m)
            r = _run(_prog("SSD2", lambda: build_SSD(2)), maps)
            yb_full = [np.concatenate([np.asarray(r[2 * b]["ybT"]), np.asarray(r[2 * b + 1]["ybT"])], axis=0) for b in range(B)]
            ysrcs = [ya_full, yb_full]
            w_out = inputs["ev_w_out"][j]
        else:
            maps = []
            for c in range(8):
                b, half = c // 2, c % 2
                m = {"hn_all": np.ascontiguousarray(hn_full[b]),
                     "pos": np.ascontiguousarray(inputs["positions"][b].reshape(1, SEQ).astype(np.int32))}
                m.update(mcon)
                m.update(mla_host_inputs(inputs, j, list(range(half * 8, half * 8 + 8))))
                maps.append(m)
            r = _run(_prog("MLA2", lambda: build_MLA(2)), maps)
            o_full = [np.concatenate([np.asarray(r[2 * b]["oT"]), np.asarray(r[2 * b + 1]["oT"])], axis=0) for b in range(B)]
            ysrcs = [o_full]
            w_out = inputs["od_w_o"][j]
        tail = "final" if layer == 3 else ("norm_gmlp" if layer == 1 else "norm")
        fi = ffn_host_inputs(inputs, layer)
        gain_next = inputs["norm_final"] if layer == 3 else inputs["norm_mix"][layer + 1]
        gi = gmlp_host_inputs(inputs, 1) if tail == "norm_gmlp" else {}
        maps = []
        for c in range(8):
            b, half = c // 2, c % 2
            m = {"hT": _halo_cols(h_full[b], half, np.float32), "flag": flag[half], "w_out": np.asarray(w_out),
                 "gain": vec128(gain_next)}
            for si, ys in enumerate(ysrcs):
                m["y%d" % si] = _halo_cols(ys[b], half, BF)
            m.update(fi)
            m.update(gi)
            maps.append(m)
        nsrc = len(ysrcs)
        r = _run(_prog("T2_%d_%s" % (nsrc, tail), lambda: build_T2(nsrc, tail)), maps)
        h_full = [np.concatenate([np.asarray(r[2 * b]["h_out"]), np.asarray(r[2 * b + 1]["h_out"])], axis=1) for b in range(B)]
        if tail == "final":
            out = np.stack([np.concatenate([np.asarray(r[2 * b]["finT"]), np.asarray(r[2 * b + 1]["finT"])], axis=1).T for b in range(B)])
        else:
            hn_full = [np.concatenate([np.asarray(r[2 * b]["hnT"]), np.asarray(r[2 * b + 1]["hnT"])], axis=1) for b in range(B)]
            if tail == "norm_gmlp":
                ya_full = [np.concatenate([np.asarray(r[2 * b]["yaT"]), np.asarray(r[2 * b + 1]["yaT"])], axis=1) for b in range(B)]
    return np.ascontiguousarray(out.astype(np.float32))
```
